# Optimizing a Trainium2 kernel written in Bass

```python
import jax
import jax.numpy as jnp
from jax import lax
import numpy as np


D_MODEL = 1024
BATCH = 16
SEQ = 2048
DEPTH = 1

SB_HEADS = 16
SB_HEAD_DIM = 64
SB_WIDTH = SB_HEADS * SB_HEAD_DIM
RET_HEADS = 4
RET_QK_DIM = 256
RET_V_DIM = 2 * RET_QK_DIM
RET_QK_WIDTH = RET_HEADS * RET_QK_DIM
RET_V_WIDTH = RET_HEADS * RET_V_DIM
N_BRANCH = 2
BLOCK = 128
D_FF = -(-8 * D_MODEL // (3 * 256)) * 256
ROPE_BASE = 10000.0
EPS = 1e-6
N_MOD = 6
IN_SPLITS = (SB_WIDTH, 2 * SB_WIDTH, 3 * SB_WIDTH,
             3 * SB_WIDTH + RET_QK_WIDTH,
             3 * SB_WIDTH + 2 * RET_QK_WIDTH,
             3 * SB_WIDTH + 2 * RET_QK_WIDTH + RET_V_WIDTH,
             3 * SB_WIDTH + 2 * RET_QK_WIDTH + 2 * RET_V_WIDTH)
IN_WIDTH = 3 * SB_WIDTH + 2 * RET_QK_WIDTH + 2 * RET_V_WIDTH + N_BRANCH * D_MODEL

kernel_name = 'hybrid_stickbreaking_retention_block'


def rmsnorm(t, g):
    tf = t.astype(jnp.float32)
    y = tf * lax.rsqrt(jnp.mean(tf * tf, axis=-1, keepdims=True) + EPS)
    return (y * g.astype(jnp.float32)).astype(t.dtype)


def head_rmsnorm(t):
    tf = t.astype(jnp.float32)
    return (tf * lax.rsqrt(jnp.mean(tf * tf, axis=-1, keepdims=True) + EPS)).astype(t.dtype)


def modulate(h, shift, scale):
    return h * (1.0 + scale[:, None, :]) + shift[:, None, :]


def rotary(t):
    s, d = t.shape[1], t.shape[-1]
    inv_freq = ROPE_BASE ** (-jnp.arange(0, d, 2, dtype=jnp.float32) / d)
    ang = jnp.arange(s, dtype=jnp.float32)[:, None] * inv_freq[None, :]
    cos = jnp.cos(ang)[None, :, None, :]
    sin = jnp.sin(ang)[None, :, None, :]
    t1, t2 = jnp.split(t.astype(jnp.float32), 2, axis=-1)
    return jnp.concatenate([t1 * cos - t2 * sin, t1 * sin + t2 * cos], axis=-1).astype(t.dtype)


def stick_breaking_attention(q, k, v):
    s_len = q.shape[2]
    scale = SB_HEAD_DIM ** -0.5
    outs = []
    for i in range(s_len // BLOCK):
        end = (i + 1) * BLOCK
        qb = q[:, :, i * BLOCK:end]
        kb = k[:, :, :end]
        vb = v[:, :, :end]
        z = jnp.einsum('bhtd,bhsd->bhts', qb, kb).astype(jnp.float32) * scale
        t_pos = i * BLOCK + jnp.arange(BLOCK)
        s_pos = jnp.arange(end)
        mask = s_pos[None, :] < t_pos[:, None]
        log_1m_beta = jnp.where(mask, jax.nn.log_sigmoid(-z), 0.0)
        after = lax.cumsum(log_1m_beta, axis=3, reverse=True) - log_1m_beta
        log_a = jax.nn.log_sigmoid(z) + after
        a = jnp.where(mask, jnp.exp(log_a), 0.0)
        outs.append(jnp.einsum('bhts,bhsd->bhtd', a.astype(v.dtype), vb))
    return jnp.concatenate(outs, axis=2)


def retention(q, k, v):
    out_dtype = v.dtype
    q = q.astype(jnp.float32)
    k = k.astype(jnp.float32)
    v = v.astype(jnp.float32)
    b, h, s_len, dk = q.shape
    dv = v.shape[-1]
    n_chunks = s_len // BLOCK
    log_gamma = jnp.log(1.0 - 2.0 ** (-5.0 - jnp.arange(h, dtype=jnp.float32)))
    idx = jnp.arange(BLOCK, dtype=jnp.float32)
    diff = idx[:, None] - idx[None, :]
    decay_mask = jnp.where(diff >= 0, jnp.exp(jnp.maximum(diff, 0.0) * log_gamma[:, None, None]), 0.0)
    query_decay = jnp.exp((idx + 1.0) * log_gamma[:, None])
    key_decay = jnp.exp((BLOCK - 1.0 - idx) * log_gamma[:, None])
    chunk_decay = jnp.exp(BLOCK * log_gamma)
    qc = q.reshape(b, h, n_chunks, BLOCK, dk)
    kc = k.reshape(b, h, n_chunks, BLOCK, dk)
    vc = v.reshape(b, h, n_chunks, BLOCK, dv)
    scores = jnp.einsum('bhncd,bhnmd->bhncm', qc, kc) * decay_mask[None, :, None]
    intra = jnp.einsum('bhncm,bhnme->bhnce', scores, vc)
    k_dec = kc * key_decay[None, :, None, :, None]

    def step(state, xs):
        q_i, k_i, v_i = xs
        cross = jnp.einsum('bhcd,bhde->bhce', q_i, state) * query_decay[None, :, :, None]
        state = state * chunk_decay[None, :, None, None] + jnp.einsum('bhcd,bhce->bhde', k_i, v_i)
        return state, cross

    xs = (jnp.moveaxis(qc, 2, 0), jnp.moveaxis(k_dec, 2, 0), jnp.moveaxis(vc, 2, 0))
    init = jnp.zeros((b, h, dk, dv), jnp.float32)
    _, cross = lax.scan(step, init, xs)
    out = intra + jnp.moveaxis(cross, 0, 2)
    return out.reshape(b, h, s_len, dv).astype(out_dtype)


def setup_inputs(seed: int = 0) -> dict:
    key = jax.random.key(seed)
    ks = jax.random.split(key, 14)

    def nrm(k, shape, fan_in):
        return jax.random.normal(k, shape, jnp.float32) * (fan_in ** -0.5)

    x = jax.random.normal(ks[0], (BATCH, SEQ, D_MODEL), jnp.float32)
    c = jax.random.normal(ks[1], (BATCH, D_MODEL), jnp.float32)
    w_ada = nrm(ks[2], (DEPTH, D_MODEL, N_MOD * D_MODEL), D_MODEL)
    b_ada = 0.02 * jax.random.normal(ks[3], (DEPTH, N_MOD * D_MODEL), jnp.float32)
    g_norm1 = 1.0 + 0.02 * jax.random.normal(ks[4], (DEPTH, D_MODEL), jnp.float32)
    w_in = nrm(ks[5], (DEPTH, D_MODEL, IN_WIDTH), D_MODEL)
    b_branch = 0.02 * jax.random.normal(ks[6], (DEPTH, N_BRANCH * D_MODEL), jnp.float32)
    w_proj_sb = nrm(ks[7], (DEPTH, SB_WIDTH, D_MODEL), SB_WIDTH)
    w_proj_ret = nrm(ks[8], (DEPTH, RET_V_WIDTH, D_MODEL), RET_V_WIDTH)
    w_out = nrm(ks[9], (DEPTH, D_MODEL, D_MODEL), D_MODEL)
    g_norm2 = 1.0 + 0.02 * jax.random.normal(ks[10], (DEPTH, D_MODEL), jnp.float32)
    w_ffn_in = nrm(ks[11], (DEPTH, D_MODEL, 2 * D_FF), D_MODEL)
    w_ffn_out = nrm(ks[12], (DEPTH, D_FF, D_MODEL), D_FF)
    g_final = 1.0 + 0.02 * jax.random.normal(ks[13], (D_MODEL,), jnp.float32)
    return {'x': x, 'c': c, 'w_ada': w_ada, 'b_ada': b_ada, 'g_norm1': g_norm1,
            'w_in': w_in, 'b_branch': b_branch, 'w_proj_sb': w_proj_sb,
            'w_proj_ret': w_proj_ret, 'w_out': w_out, 'g_norm2': g_norm2,
            'w_ffn_in': w_ffn_in, 'w_ffn_out': w_ffn_out, 'g_final': g_final}


def reference(x, c, w_ada, b_ada, g_norm1, w_in, b_branch, w_proj_sb, w_proj_ret,
              w_out, g_norm2, w_ffn_in, w_ffn_out, g_final):
    b, s_len, _ = x.shape
    c_act = jax.nn.silu(c)
    for l in range(DEPTH):
        mod = jnp.einsum('bd,de->be', c_act, w_ada[l]) + b_ada[l]
        shift1, scale1, gate1, shift2, scale2, gate2 = jnp.split(mod, N_MOD, axis=-1)

        hmix = modulate(rmsnorm(x, g_norm1[l]), shift1, scale1)
        proj = jnp.einsum('bsd,de->bse', hmix, w_in[l])
        sb_q, sb_k, sb_v, r_q, r_k, r_v, r_g, br = jnp.split(proj, IN_SPLITS, axis=-1)

        def sb_heads(t):
            return t.reshape(b, s_len, SB_HEADS, SB_HEAD_DIM).transpose(0, 2, 1, 3)
        o_sb = stick_breaking_attention(sb_heads(sb_q), sb_heads(sb_k), sb_heads(sb_v))
        o_sb = o_sb.transpose(0, 2, 1, 3).reshape(b, s_len, SB_WIDTH)

        rq = rotary(r_q.reshape(b, s_len, RET_HEADS, RET_QK_DIM))
        rk = rotary(r_k.reshape(b, s_len, RET_HEADS, RET_QK_DIM)) * (RET_QK_DIM ** -0.5)
        rv = r_v.reshape(b, s_len, RET_HEADS, RET_V_DIM)
        o_ret = retention(rq.transpose(0, 2, 1, 3), rk.transpose(0, 2, 1, 3), rv.transpose(0, 2, 1, 3))
        o_ret = head_rmsnorm(o_ret.transpose(0, 2, 1, 3)).reshape(b, s_len, RET_V_WIDTH)
        o_ret = jax.nn.silu(r_g) * o_ret

        g_sb, g_ret = jnp.split(jax.nn.sigmoid(br + b_branch[l]), N_BRANCH, axis=-1)
        merged = (g_sb * jnp.einsum('bse,ed->bsd', o_sb, w_proj_sb[l])
                  + g_ret * jnp.einsum('bse,ed->bsd', o_ret, w_proj_ret[l]))
        x = x + gate1[:, None, :] * jnp.einsum('bsd,de->bse', merged, w_out[l])

        hffn = modulate(rmsnorm(x, g_norm2[l]), shift2, scale2)
        a, u = jnp.split(jnp.einsum('bsd,df->bsf', hffn, w_ffn_in[l]), 2, axis=-1)
        x = x + gate2[:, None, :] * jnp.einsum('bsf,fd->bsd', jax.nn.silu(a) * u, w_ffn_out[l])

    return rmsnorm(x, g_final)
```

```python
import numpy as np
import ml_dtypes
from contextlib import ExitStack
import concourse.bass as bass
import concourse.mybir as mybir
from concourse.bass_utils import run_bass_kernel_spmd

F32 = mybir.dt.float32
BF16 = mybir.dt.bfloat16
AF = mybir.ActivationFunctionType
ALU = mybir.AluOpType

D = 1024
S = 2048
NB = 2
NCORES = 8
DFF = 2816
EPS = 1e-6
KB = 1024
ARENA_BYTES = 198 * KB
import os
DEBUG = os.environ.get('KDEBUG') == '1'
STOP = os.environ.get('KSTOP', '')


class _Stop(Exception):
    pass


class Reg:
    __slots__ = ("name", "w", "r", "dsem", "dcnt", "excl")

    def __init__(self, name, excl=False):
        self.name = name
        self.excl = excl
        self.w = None
        self.r = {}
        self.dsem = None
        self.dcnt = 0


class Eng:
    def __init__(self, name, eng, sem):
        self.name = name
        self.eng = eng
        self.sem = sem
        self.count = 0
        self.seen = {}


class Sched:
    def __init__(self, nc, es):
        self.nc = nc
        self.es = es
        self.E = {}
        for name, e in (("pe", nc.tensor), ("act", nc.scalar), ("dve", nc.vector),
                        ("pool", nc.gpsimd), ("sp", nc.sync)):
            self.E[name] = Eng(name, e, es.enter_context(nc.semaphore("sem_" + name)))
        self.pending = []
        self.nops = 0
        self.stopped = False
        self.limit = int(os.environ.get('KSTOPN', '0'))

    def _tick(self):
        self.nops += 1
        if self.limit and self.nops == self.limit:
            self.barrier()
            self.stopped = True

    def _deps(self, reads, writes, en=None):
        d = {}

        def add(ev):
            if ev is None:
                return
            key, sem, val = ev
            if key not in d or d[key][1] < val:
                d[key] = (sem, val)
        for r in reads:
            add(r.w)
            if r.excl:
                for ev in r.r.values():
                    if ev[0] != en:
                        add(ev)
        for w in writes:
            add(w.w)
            for ev in w.r.values():
                add(ev)
        return d

    def _wait(self, E, d):
        for key, (sem, val) in d.items():
            if E.name == "pe" and key == "pe":
                continue
            if E.seen.get(key, 0) < val:
                E.eng.wait_ge(sem, val)
                E.seen[key] = val

    def op(self, en, fn, reads=(), writes=(), inc=True):
        if self.stopped:
            return None
        E = self.E[en]
        self._wait(E, self._deps(reads, writes, en))
        ins = fn(E.eng)
        if inc:
            ins.then_inc(E.sem, 1)
            E.count += 1
            val = E.count
        else:
            val = E.count + 1
        ev = (en, E.sem, val)
        for r in reads:
            r.r[en] = ev
        for w in writes:
            w.w = ev
            w.r = {}
        self._tick()
        return ins

    def dma(self, qn, out_ap, in_ap, reads=(), writes=(), tgt=None, **kw):
        if self.stopped:
            return
        E = self.E[qn]
        self._wait(E, self._deps(reads, writes))
        if tgt.dsem is None:
            tgt.dsem = self.es.enter_context(self.nc.semaphore("d_" + tgt.name))
        tgt.dcnt += 16
        E.eng.dma_start(out=out_ap, in_=in_ap, **kw).then_inc(tgt.dsem, 16)
        ev = ("d_" + tgt.name, tgt.dsem, tgt.dcnt)
        for r in reads:
            r.r[ev[0]] = ev
        for w in writes:
            w.w = ev
            w.r = {}
        self.pending.append(ev)
        self._tick()

    def barrier(self):
        if self.stopped:
            return
        evs = [(n, E.sem, E.count) for n, E in self.E.items() if E.count > 0] + self.pending
        d = {}
        for key, sem, val in evs:
            if key not in d or d[key][1] < val:
                d[key] = (sem, val)
        for E in self.E.values():
            self._wait(E, d)
        self.pending = []


def build_program():
    nc = bass.Bass("TRN2", target_bir_lowering=False)

    def din(name, shape, dt=F32):
        return nc.dram_tensor(name, list(shape), dt, kind="ExternalInput").ap()

    x_d = din("x", [NB, S, D])
    cT_d = din("cT", [128, 8, NB])
    w_ada_d = din("w_ada", [D, 6 * D])
    badaT2_d = din("badaT2", [128, 96])
    gT_d = din("gT", [128, 16])
    bbrT_d = din("bbrT", [128, 16])
    w_in_d = din("w_in", [D, 11264])
    w_psb_d = din("w_proj_sb", [D, D])
    w_pret_d = din("w_proj_ret", [2 * D, D])
    w_out_d = din("w_out", [D, D])
    w_f1_d = din("w_ffn_in", [D, 2 * DFF])
    w_f2_d = din("w_ffn_out", [DFF, D])
    identF_d = din("identF", [128, 2, 128])
    cb16_d = din("cb16", [128, 4, 128], BF16)
    cossin_d = din("cossin", [128, 2, S])
    dmaskT_d = din("dmaskT", [128, 4, 128])
    rdec_d = din("rdec", [128, 8])
    gfin_d = din("gfin", [128, D])
    out_d = nc.dram_tensor("out", [NB, S, D], F32, kind="ExternalOutput").ap()
    dbg = {}
    if DEBUG:
        for nm in ("hmix", "mret", "osb", "merged", "hffn"):
            dbg[nm] = nc.dram_tensor("dbg_" + nm, [128, 8, S], BF16, kind="ExternalOutput").ap()
        dbg["x1"] = nc.dram_tensor("dbg_x1", [128, 16, D], F32, kind="ExternalOutput").ap()
        dbg["mod"] = nc.dram_tensor("dbg_mod", [128, 96], F32, kind="ExternalOutput").ap()
        dbg["qT"] = nc.dram_tensor("dbg_qT", [128, 2, S], BF16, kind="ExternalOutput").ap()
        dbg["kT"] = nc.dram_tensor("dbg_kT", [128, 2, S], BF16, kind="ExternalOutput").ap()
        dbg["oT"] = nc.dram_tensor("dbg_oT", [128, 8, S], BF16, kind="ExternalOutput").ap()

    log_gamma = [float(np.log(1.0 - 2.0 ** (-5.0 - h))) for h in range(4)]
    chunk_decay = [float(np.exp(128.0 * lg)) for lg in log_gamma]

    es = ExitStack()
    with es:
        k = Sched(nc, es)

        def sb(name, shape, dt):
            return es.enter_context(nc.sbuf_tensor("s_" + name, list(shape), dt))

        arena = sb("arena", [128, ARENA_BYTES // 2], BF16)
        identF2 = sb("identF", [128, 2, 128], F32)
        identF = identF2[:, 0, :]
        onesF = identF2[:, 1, :]
        cb16 = sb("cb16", [128, 4, 128], BF16)
        dmaskT = sb("dmaskT", [128, 4, 128], F32)
        rdec = sb("rdec", [128, 8], F32)
        cTs = sb("cTs", [128, 8, NB], F32)
        cact = sb("cact", [128, 8, NB], BF16)
        badaT2 = sb("badaT2", [128, 96], F32)
        gT = sb("gT", [128, 16], F32)
        bbrT = sb("bbrT", [128, 16], F32)
        modT = sb("modT", [128, 96], F32)
        scs = sb("scs", [128, NB, 2, 8], F32)
        dgt = sb("dgt", [128, 2, 128], F32)
        small = sb("small", [128, 64], F32)
        ps = [es.enter_context(nc.psum_tensor(f"ps{i}", [128, 512], F32)) for i in range(8)]
        Rps = [Reg(f"ps{i}", excl=True) for i in range(8)]
        psb = [p[:].bitcast(BF16) for p in ps]
        identB = cb16[:, 0, :]
        NLm = cb16[:, 1, :]
        NEG1 = cb16[:, 2, :]
        mask01 = cb16[:, 3, :]
        modT3 = modT[:].rearrange("p (c b) -> p c b", b=NB)

        def view(off, shape, dt):
            n = int(np.prod(shape[1:]))
            esz = 2 if dt == BF16 else 4
            assert off % 4 == 0 and off + n * esz <= ARENA_BYTES, (off, shape)
            a = arena[:, off // 2: off // 2 + n * esz // 2]
            if dt == F32:
                a = a.bitcast(F32)
            if len(shape) == 3:
                a = a.rearrange("p (a b) -> p a b", a=shape[1])
            elif len(shape) == 4:
                a = a.rearrange("p (a b c) -> p a b c", a=shape[1], b=shape[2])
            return a

        Rc = Reg("consts")
        Rout = Reg("outdma")
        Rdbg = Reg("dbgdma")

        def dump(nm, ap, b, stage):
            if DEBUG and b == 0:
                k.barrier()
                k.dma("sp", dbg[nm], ap, tgt=Rdbg)
                k.barrier()
            if STOP == stage:
                k.barrier()
                raise _Stop()
        for dst, src in ((identF2[:], identF_d), (cb16[:], cb16_d), (dmaskT[:], dmaskT_d), (rdec[:], rdec_d),
                         (cTs[:], cT_d), (badaT2[:], badaT2_d),
                         (gT[:], gT_d), (bbrT[:], bbrT_d)):
            k.dma("sp", dst, src, writes=[Rc], tgt=Rc)

        if STOP == 'init':
            k.barrier()
            return nc

        def MM(out, lhsT, rhs, start, stop, reads, writes, inc=None, **kw):
            if inc is None:
                inc = stop
            k.op("pe", lambda e: e.matmul(out, lhsT=lhsT, rhs=rhs, start=start, stop=stop, **kw),
                 reads, writes, inc)

        def wview(src2d):
            return src2d.rearrange("(kc p) n -> p kc n", p=128)

        hT = view(0, [128, 8, S], BF16)
        RhT = [[Reg(f"hT{kc}_{n}") for n in range(16)] for kc in range(8)]
        mT = view(32 * KB, [128, 8, S], BF16)
        RmT = [[Reg(f"mT{ec}_{tg}") for tg in range(4)] for ec in range(8)]

        def hregs(kc, n0, n1):
            return RhT[kc][n0:n1]

        small_i = [0]

        def stmp():
            i = small_i[0] % 64
            small_i[0] += 1
            return small[:, i:i + 1], Reg(f"sm{small_i[0]}")

        Rcact = Reg("cact")
        k.op("act", lambda e: e.activation(out=cact[:], in_=cTs[:], func=AF.Silu), [Rc], [Rcact])
        wblk = [view(64 * KB + i * 8 * KB, [128, 8, 512], BF16) for i in range(3)]
        Rw = [Reg(f"wblk{i}") for i in range(3)]
        Rmod = Reg("modT")
        pm, Rpm = ps[0], Rps[0]
        for j in range(12):
            b_ = j % 3
            k.dma("pool", wblk[b_], wview(w_ada_d[:, j * 512:(j + 1) * 512]), writes=[Rw[b_]], tgt=Rw[b_])
            for ec in range(4):
                c0 = (j * 4 + ec) * NB
                for kc in range(8):
                    MM(pm[:, c0:c0 + NB], wblk[b_][:, kc, ec * 128:(ec + 1) * 128], cact[:, kc, :],
                       kc == 0, kc == 7, [Rw[b_], Rcact], [Rpm])
        k.op("dve", lambda e: e.tensor_tensor(out=modT[:], in0=pm[:, 0:96], in1=badaT2[:], op=ALU.add),
             [Rpm, Rc], [Rmod])
        Rscs = Reg("scs")
        for b in range(NB):
            for wi in range(2):
                k.op("dve", lambda e, b=b, wi=wi: e.scalar_tensor_tensor(
                    out=scs[:, b, wi, :], in0=modT3[:, 8 + 24 * wi:16 + 24 * wi, b], scalar=1.0,
                    in1=gT[:, wi * 8:(wi + 1) * 8], op0=ALU.add, op1=ALU.mult), [Rmod, Rc], [Rscs])
        k.barrier()
        if STOP == 'mod':
            return nc
        print('nops after mod', k.nops)

        def rms_rstd(src_ap, Rsrc, junk_ap, Rjunk, width):
            a1, R1 = stmp()
            a2, R2 = stmp()
            a3, R3 = stmp()
            k.op("act", lambda e: e.activation(out=junk_ap, in_=src_ap, func=AF.Square, accum_out=a1),
                 [Rsrc], [Rjunk, R1])
            k.op("dve", lambda e: e.tensor_scalar(out=a2, in0=a1, scalar1=1.0 / width, scalar2=EPS,
                                                  op0=ALU.mult, op1=ALU.add), [R1], [R2])
            k.op("act", lambda e: e.activation(out=a2, in_=a2, func=AF.Sqrt), [R2], [R2])
            k.op("dve", lambda e: e.reciprocal(out=a3, in_=a2), [R2], [R3])
            return a3, R3

        def norm_to_hT(xt_ap, Rx, n, b, wi, xn_ap, Rxn, junk_ap, Rjunk, pbanks):
            rstd, Rr = rms_rstd(xt_ap, Rx, junk_ap, Rjunk, D)
            k.op("dve", lambda e: e.tensor_scalar(out=xn_ap, in0=xt_ap, scalar1=rstd, scalar2=None, op0=ALU.mult),
                 [Rx, Rr], [Rxn])
            for kc in range(8):
                pi = pbanks[kc // 4]
                q = kc % 4
                k.op("pe", lambda e, pi=pi, q=q, kc=kc: e.transpose(
                    ps[pi][:, q * 128:(q + 1) * 128], xn_ap[:, kc * 128:(kc + 1) * 128], identF),
                    [Rxn, Rc], [Rps[pi]])
            for kc in range(8):
                pi = pbanks[kc // 4]
                q = kc % 4
                o_ap = hT[:, kc, n * 128:(n + 1) * 128]
                i_ap = ps[pi][:, q * 128:(q + 1) * 128]
                sc_ap = scs[:, b, wi, kc:kc + 1]
                sh_ap = modT3[:, 24 * wi + kc, b:b + 1]
                if kc // 4 == 0:
                    k.op("act", lambda e, o_ap=o_ap, i_ap=i_ap, sc_ap=sc_ap, sh_ap=sh_ap: e.activation(
                        out=o_ap, in_=i_ap, func=AF.Identity, scale=sc_ap, bias=sh_ap),
                        [Rps[pi], Rscs, Rmod], [RhT[kc][n]])
                else:
                    k.op("dve", lambda e, o_ap=o_ap, i_ap=i_ap, sc_ap=sc_ap, sh_ap=sh_ap: e.tensor_scalar(
                        out=o_ap, in0=i_ap, scalar1=sc_ap, scalar2=sh_ap, op0=ALU.mult, op1=ALU.add),
                        [Rps[pi], Rscs, Rmod], [RhT[kc][n]])

        def main_loop():
          for b in range(NB):
            xbuf = [view(64 * KB + i * 4 * KB, [128, D], F32) for i in range(2)]
            Rxb = [Reg(f"xb{b}_{i}") for i in range(2)]
            xnb = [view(72 * KB + i * 4 * KB, [128, D], F32) for i in range(2)]
            Rxn = [Reg(f"xn{b}_{i}") for i in range(2)]
            junk = view(80 * KB, [128, D], BF16)
            Rjunk = Reg(f"junk{b}")
            for n in range(16):
                i = n % 2
                k.dma("sp", xbuf[i], x_d[b, n * 128:(n + 1) * 128, :], writes=[Rxb[i]], tgt=Rxb[i])
                norm_to_hT(xbuf[i], Rxb[i], n, b, 0, xnb[i], Rxn[i], junk, Rjunk, (2 * i, 2 * i + 1))
            k.barrier()
            if DEBUG and b == 0:
                k.dma('sp', dbg['mod'], modT[:], tgt=Rdbg)
            print('nops after A', b, k.nops)
            dump('hmix', hT, b, 'A')

            oT = view(64 * KB, [128, 8, S], BF16)
            Ror = [Reg(f"or{b}_{n}") for n in range(16)]
            cossin = view(96 * KB, [128, 2, S], F32)
            Rcs = Reg(f"cossin{b}")
            wqk = view(112 * KB, [128, 8, 512], BF16)
            Rwqk = Reg(f"wqk{b}")
            wvg = [view(120 * KB + i * 16 * KB, [128, 8, 1024], BF16) for i in range(2)]
            Rwvg = [Reg(f"wvg{b}_{i}") for i in range(2)]
            qT = view(152 * KB, [128, 2, S], BF16)
            kT = view(160 * KB, [128, 2, S], BF16)
            RqT = [Reg(f"qT{b}_{tg}") for tg in range(4)]
            RkT = [Reg(f"kT{b}_{tg}") for tg in range(4)]
            st32 = view(168 * KB, [128, 2, 512], F32)
            stb = view(172 * KB, [128, 2, 512], BF16)
            Rst32, Rstb = Reg(f"st32{b}"), Reg(f"stb{b}")
            rt = [view(174 * KB + i * 2 * KB, [128, 512], F32) for i in range(4)]
            Rrt = [Reg(f"rt{b}_{i}") for i in range(4)]
            o = 182 * KB
            vn = [view(o + i * KB, [128, 512], BF16) for i in range(2)]
            sgn = [view(o + 2 * KB + i * KB, [128, 512], BF16) for i in range(2)]
            kdn = [view(o + 4 * KB + i * 512, [128, 256], BF16) for i in range(2)]
            scT = [view(o + 5 * KB + i * 256, [128, 128], BF16) for i in range(2)]
            o32 = [view(o + 6 * KB + i * 2 * KB, [128, 512], F32) for i in range(2)]
            og = [view(o + 10 * KB + i * KB, [128, 512], BF16) for i in range(2)]
            rjunk = view(o + 12 * KB, [128, 512], BF16)
            Rvn = [Reg(f"vn{b}_{i}") for i in range(2)]
            Rsgn = [Reg(f"sgn{b}_{i}") for i in range(2)]
            Rkdn = [Reg(f"kdn{b}_{i}") for i in range(2)]
            RscT = [Reg(f"scT{b}_{i}") for i in range(2)]
            Ro32 = [Reg(f"o32{b}_{i}") for i in range(2)]
            Rog = [Reg(f"og{b}_{i}") for i in range(2)]
            Rrj = Reg(f"rjunk{b}")
            k.dma("sp", cossin, cossin_d, writes=[Rcs], tgt=Rcs)
            cosT, sinT = cossin[:, 0, :], cossin[:, 1, :]

            for half in range(2):
                for hl in range(2):
                    h = half * 2 + hl
                    k.dma("pool", wqk[:, :, 0:256], wview(w_in_d[:, 3072 + h * 256:3072 + (h + 1) * 256]),
                          writes=[Rwqk], tgt=Rwqk)
                    k.dma("pool", wqk[:, :, 256:512], wview(w_in_d[:, 4096 + h * 256:4096 + (h + 1) * 256]),
                          writes=[Rwqk], tgt=Rwqk)
                    wv_, Rwv_ = wvg[h % 2], Rwvg[h % 2]
                    k.dma("pool", wv_[:, :, 0:512], wview(w_in_d[:, 5120 + h * 512:5120 + (h + 1) * 512]),
                          writes=[Rwv_], tgt=Rwv_)
                    k.dma("pool", wv_[:, :, 512:1024], wview(w_in_d[:, 7168 + h * 512:7168 + (h + 1) * 512]),
                          writes=[Rwv_], tgt=Rwv_)
                    for tg in range(4):
                        for which in range(2):
                            dstT, Rd = (qT, RqT[tg]) if which == 0 else (kT, RkT[tg])
                            for dc in range(2):
                                for kc in range(8):
                                    c0 = which * 256 + dc * 128
                                    MM(ps[dc][:, :], wqk[:, kc, c0:c0 + 128], hT[:, kc, tg * 512:(tg + 1) * 512],
                                       kc == 0, kc == 7, [Rwqk] + hregs(kc, tg * 4, tg * 4 + 4), [Rps[dc]])
                            A_, B_ = ps[0][:, :], ps[1][:, :]
                            cs = cosT[:, tg * 512:(tg + 1) * 512]
                            sn = sinT[:, tg * 512:(tg + 1) * 512]
                            k.op("dve", lambda e, A_=A_, cs=cs: e.tensor_tensor(out=rt[0], in0=A_, in1=cs, op=ALU.mult),
                                 [Rps[0], Rcs], [Rrt[0]])
                            k.op("dve", lambda e, B_=B_, sn=sn: e.tensor_tensor(out=rt[1], in0=B_, in1=sn, op=ALU.mult),
                                 [Rps[1], Rcs], [Rrt[1]])
                            k.op("dve", lambda e, A_=A_, sn=sn: e.tensor_tensor(out=rt[2], in0=A_, in1=sn, op=ALU.mult),
                                 [Rps[0], Rcs], [Rrt[2]])
                            k.op("dve", lambda e, B_=B_, cs=cs: e.tensor_tensor(out=rt[3], in0=B_, in1=cs, op=ALU.mult),
                                 [Rps[1], Rcs], [Rrt[3]])
                            d1 = dstT[:, 0, tg * 512:(tg + 1) * 512]
                            d2 = dstT[:, 1, tg * 512:(tg + 1) * 512]
                            k.op("pool", lambda e, d1=d1: e.tensor_tensor(out=d1, in0=rt[0], in1=rt[1], op=ALU.subtract),
                                 [Rrt[0], Rrt[1]], [Rd])
                            k.op("pool", lambda e, d2=d2: e.tensor_tensor(out=d2, in0=rt[2], in1=rt[3], op=ALU.add),
                                 [Rrt[2], Rrt[3]], [Rd])
                    if h == 0:
                        dump('qT', qT, b, 'q')
                        dump('kT', kT, b, 'k')
                    k.op("pool", lambda e: e.memset(st32, 0.0), [], [Rst32])
                    k.op("pool", lambda e: e.memset(stb, 0.0), [], [Rstb])
                    for n in range(16):
                        i = n % 2
                        tg = n // 4
                        tsl = slice(n * 128, (n + 1) * 128)
                        for kc in range(8):
                            MM(ps[0][:, :], hT[:, kc, tsl], wv_[:, kc, 0:512], kc == 0, kc == 7,
                               [Rwv_, RhT[kc][n]], [Rps[0]])
                        k.op("act", lambda e, i=i: e.copy(out=vn[i], in_=ps[0][:, :]), [Rps[0]], [Rvn[i]])
                        for kc in range(8):
                            MM(ps[1][:, :], hT[:, kc, tsl], wv_[:, kc, 512:1024], kc == 0, kc == 7,
                               [Rwv_, RhT[kc][n]], [Rps[1]])
                        k.op("act", lambda e, i=i: e.activation(out=sgn[i], in_=ps[1][:, :], func=AF.Silu),
                             [Rps[1]], [Rsgn[i]])
                        for dc in range(2):
                            k.op("pe", lambda e, dc=dc: e.transpose(
                                psb[2][:, dc * 128:(dc + 1) * 128], kT[:, dc, tsl], identB),
                                [RkT[tg], Rc], [Rps[2]])
                        k.op("dve", lambda e, i=i, h=h: e.tensor_scalar(
                            out=kdn[i], in0=psb[2][:, 0:256], scalar1=rdec[:, 4 + h:5 + h], scalar2=None,
                            op0=ALU.mult), [Rps[2], Rc], [Rkdn[i]])
                        for dc in range(2):
                            MM(ps[3][:, 0:128], kT[:, dc, tsl], qT[:, dc, tsl], dc == 0, dc == 1,
                               [RkT[tg], RqT[tg]], [Rps[3]])
                        k.op("dve", lambda e, i=i, h=h: e.tensor_tensor(
                            out=scT[i], in0=ps[3][:, 0:128], in1=dmaskT[:, h, :], op=ALU.mult),
                            [Rps[3], Rc], [RscT[i]])
                        MM(ps[4][:, :], scT[i], vn[i], True, True, [RscT[i], Rvn[i]], [Rps[4]])
                        k.op("act", lambda e, i=i: e.copy(out=o32[i], in_=ps[4][:, :]), [Rps[4]], [Ro32[i]])
                        if n > 0:
                            for dc in range(2):
                                MM(ps[5][:, :], qT[:, dc, tsl], stb[:, dc, :], dc == 0, dc == 1,
                                   [RqT[tg], Rstb], [Rps[5]])
                            k.op("dve", lambda e, i=i, h=h: e.scalar_tensor_tensor(
                                out=o32[i], in0=ps[5][:, :], scalar=rdec[:, h:h + 1], in1=o32[i],
                                op0=ALU.mult, op1=ALU.add), [Rps[5], Rc, Ro32[i]], [Ro32[i]])
                        rstd, Rr = rms_rstd(o32[i], Ro32[i], rjunk, Rrj, 512)
                        k.op("dve", lambda e, i=i, rstd=rstd: e.scalar_tensor_tensor(
                            out=og[i], in0=o32[i], scalar=rstd, in1=sgn[i], op0=ALU.mult, op1=ALU.mult),
                            [Ro32[i], Rr, Rsgn[i]], [Rog[i]])
                        for j in range(4):
                            k.op("pe", lambda e, i=i, j=j: e.transpose(
                                psb[6][:, j * 128:(j + 1) * 128], og[i][:, j * 128:(j + 1) * 128], identB),
                                [Rog[i], Rc], [Rps[6]])
                        k.op("act", lambda e, hl=hl, tsl=tsl: e.copy(
                            out=oT[:, hl * 4:(hl + 1) * 4, tsl],
                            in_=psb[6][:, 0:512].rearrange("p (a b) -> p a b", a=4)), [Rps[6]], [Ror[n]])
                        if n < 15:
                            for dc in range(2):
                                MM(ps[7][:, :], kdn[i][:, dc * 128:(dc + 1) * 128], vn[i], True, True,
                                   [Rkdn[i], Rvn[i]], [Rps[7]])
                                k.op("dve", lambda e, dc=dc, h=h: e.scalar_tensor_tensor(
                                    out=st32[:, dc, :], in0=st32[:, dc, :], scalar=chunk_decay[h], in1=ps[7][:, :],
                                    op0=ALU.mult, op1=ALU.add), [Rps[7], Rst32], [Rst32])
                            k.op("pool", lambda e: e.tensor_copy(out=stb, in_=st32), [Rst32], [Rstb])
                        k.barrier()
                k.barrier()
                if half == 0:
                    dump('oT', oT, b, 'R0')
                wpr = view(112 * KB, [128, 8, D], BF16)
                wbr = view(128 * KB, [128, 8, D], BF16)
                gtmp = [view(144 * KB + i * 2 * KB, [128, 512], F32) for i in range(2)]
                ptmp = [view(148 * KB + i * 2 * KB, [128, 512], F32) for i in range(2)]
                Rwpr, Rwbr = Reg(f"wpr{b}_{half}"), Reg(f"wbrr{b}_{half}")
                Rgt = [Reg(f"gt{b}_{half}_{i}") for i in range(2)]
                Rpt = [Reg(f"pt{b}_{half}_{i}") for i in range(2)]
                for q in range(2):
                    r0 = half * 1024 + q * 512
                    k.dma("pool", wpr[:, q * 4:(q + 1) * 4, :],
                          w_pret_d[r0:r0 + 512, :].rearrange("(fc p) n -> p fc n", p=128),
                          writes=[Rwpr], tgt=Rwpr)
                    k.dma("pool", wbr[:, :, q * 512:(q + 1) * 512],
                          wview(w_in_d[:, 10240 + q * 512:10240 + (q + 1) * 512]), writes=[Rwbr], tgt=Rwbr)
                it = 0
                for ec in range(8):
                    for tg in range(4):
                        i = it % 2
                        it += 1
                        P_, G_ = 2 * i, 2 * i + 1
                        tsl = slice(tg * 512, (tg + 1) * 512)
                        for fc in range(8):
                            MM(ps[P_][:, :], wpr[:, fc, ec * 128:(ec + 1) * 128], oT[:, fc, tsl], fc == 0, fc == 7,
                               [Rwpr] + Ror[tg * 4:tg * 4 + 4], [Rps[P_]])
                        for kc in range(8):
                            MM(ps[G_][:, :], wbr[:, kc, ec * 128:(ec + 1) * 128], hT[:, kc, tsl], kc == 0, kc == 7,
                               [Rwbr] + hregs(kc, tg * 4, tg * 4 + 4), [Rps[G_]])
                        k.op("act", lambda e, i=i, G_=G_, ec=ec: e.activation(
                            out=gtmp[i], in_=ps[G_][:, :], func=AF.Sigmoid, bias=bbrT[:, 8 + ec:9 + ec]),
                            [Rps[G_], Rc], [Rgt[i]])
                        if half == 0:
                            k.op("dve", lambda e, i=i, P_=P_, ec=ec, tsl=tsl: e.tensor_tensor(
                                out=mT[:, ec, tsl], in0=ps[P_][:, :], in1=gtmp[i], op=ALU.mult),
                                [Rps[P_], Rgt[i]], [RmT[ec][tg]])
                        else:
                            k.op("dve", lambda e, i=i, P_=P_: e.tensor_tensor(
                                out=ptmp[i], in0=ps[P_][:, :], in1=gtmp[i], op=ALU.mult),
                                [Rps[P_], Rgt[i]], [Rpt[i]])
                            k.op("pool", lambda e, i=i, ec=ec, tsl=tsl: e.tensor_tensor(
                                out=mT[:, ec, tsl], in0=mT[:, ec, tsl], in1=ptmp[i], op=ALU.add),
                                [Rpt[i], RmT[ec][tg]], [RmT[ec][tg]])
                k.barrier()

            dump('mret', mT, b, 'C')
            osT = view(64 * KB, [128, 8, S], BF16)
            Ros = [[Reg(f"os{b}_{hp}_{Q}") for Q in range(4)] for hp in range(8)]
            o = 96 * KB
            w3 = [view(o + i * 6 * KB, [128, 8, 3, 128], BF16) for i in range(2)]
            qT2 = [view(o + 12 * KB + i * 4 * KB, [128, S], BF16) for i in range(2)]
            kT2 = [view(o + 20 * KB + i * 4 * KB, [128, S], BF16) for i in range(2)]
            Vp = [view(o + 28 * KB + i * 4 * KB, [128, 16, 128], BF16) for i in range(2)]
            e32 = [view(o + 36 * KB + i * 2 * KB, [128, 512], F32) for i in range(3)]
            lpb = [view(o + 42 * KB + i * KB, [128, 512], BF16) for i in range(3)]
            Sb = [view(o + 45 * KB + i * KB, [128, 512], BF16) for i in range(2)]
            ATb = [view(o + 47 * KB + i * KB, [128, 512], BF16) for i in range(3)]
            ex32 = [view(o + 50 * KB + i * 2 * KB, [128, 512], F32) for i in range(2)]
            Rw3 = [Reg(f"w3{b}_{i}") for i in range(2)]
            RqT2 = [Reg(f"qT2{b}_{i}") for i in range(2)]
            RkT2 = [Reg(f"kT2{b}_{i}") for i in range(2)]
            RVp = [Reg(f"Vp{b}_{i}") for i in range(2)]
            Re32 = [Reg(f"e32{b}_{i}") for i in range(3)]
            Rex = [Reg(f"ex32{b}_{i}") for i in range(2)]
            Rlp = [Reg(f"lp{b}_{i}") for i in range(3)]
            RSb = [Reg(f"Sb{b}_{i}") for i in range(2)]
            RAT = [Reg(f"AT{b}_{i}") for i in range(3)]
            wps = view(152 * KB, [128, 8, D], BF16)
            wbs = view(168 * KB, [128, 8, D], BF16)
            Rwps, Rwbs = Reg(f"wps{b}"), Reg(f"wbs{b}")

            tiles = []
            row_id = 0
            for hp in range(8):
                for Q in range(4):
                    for hh in range(2):
                        kbs = list(range(4 * Q + 3, -1, -1))
                        for kb in kbs:
                            tiles.append(dict(hp=hp, Q=Q, hh=hh, kb=kb, first=(kb == kbs[0]), last=(kb == 0),
                                              row=row_id))
                        row_id += 1
            proj_done = set()

            def sb_proj(hp):
                i = hp % 2
                for c3 in range(3):
                    k.dma("pool", w3[i][:, :, c3, :], wview(w_in_d[:, c3 * 1024 + hp * 128:c3 * 1024 + (hp + 1) * 128]),
                          writes=[Rw3[i]], tgt=Rw3[i])
                for tg in range(4):
                    tsl = slice(tg * 512, (tg + 1) * 512)
                    for which in range(2):
                        pb_ = 6 + which
                        for kc in range(8):
                            MM(ps[pb_][:, :], w3[i][:, kc, which, :], hT[:, kc, tsl], kc == 0, kc == 7,
                               [Rw3[i]] + hregs(kc, tg * 4, tg * 4 + 4), [Rps[pb_]])
                        if which == 0:
                            k.op("act", lambda e, i=i, tsl=tsl, pb_=pb_: e.activation(
                                out=qT2[i][:, tsl], in_=ps[pb_][:, :], func=AF.Copy, scale=0.125),
                                [Rps[pb_]], [RqT2[i]])
                        else:
                            k.op("dve", lambda e, i=i, tsl=tsl, pb_=pb_: e.tensor_copy(
                                out=kT2[i][:, tsl], in_=ps[pb_][:, :]), [Rps[pb_]], [RkT2[i]])
                for sg in range(4):
                    pb_ = 6 + sg % 2
                    for j in range(4):
                        sblk = sg * 4 + j
                        for kc in range(8):
                            MM(ps[pb_][:, j * 128:(j + 1) * 128], hT[:, kc, sblk * 128:(sblk + 1) * 128],
                               w3[i][:, kc, 2, :], kc == 0, kc == 7, [Rw3[i], RhT[kc][sblk]], [Rps[pb_]])
                    k.op("dve", lambda e, i=i, sg=sg, pb_=pb_: e.tensor_copy(
                        out=Vp[i][:, sg * 4:(sg + 1) * 4, :],
                        in_=ps[pb_][:, :].rearrange("p (a b) -> p a b", a=4)), [Rps[pb_]], [RVp[i]])

            def geom(t):
                Q, kb = t["Q"], t["kb"]
                off = max(0, kb * 128 - Q * 512)
                return off, 512 - off, Q * 512 + off, kb >= 4 * Q

            def stageA(ti, t):
                hp, hh, kb = t["hp"], t["hh"], t["kb"]
                i = hp % 2
                if hp not in proj_done:
                    sb_proj(hp)
                    proj_done.add(hp)
                off, Wd, t0, diag = geom(t)
                r0 = hh * 64
                zb = ti % 4
                t["zb"], t["e"], t["lp"] = zb, ti % 3, ti % 3
                ei, li = t["e"], t["lp"]
                MM(ps[zb][:, 0:Wd], kT2[i][r0:r0 + 64, kb * 128:(kb + 1) * 128], qT2[i][r0:r0 + 64, t0:t0 + Wd],
                   True, True, [RkT2[i], RqT2[i]], [Rps[zb]])
                k.op("act", lambda e: e.activation(out=e32[ei][:, 0:Wd], in_=ps[zb][:, 0:Wd], func=AF.Exp),
                     [Rps[zb]], [Re32[ei]])
                if diag:
                    k.op("dve", lambda e: e.tensor_tensor(out=e32[ei][:, 0:128], in0=e32[ei][:, 0:128], in1=mask01,
                                                          op=ALU.mult), [Re32[ei], Rc], [Re32[ei]])
                k.op("act", lambda e: e.activation(out=lpb[li][:, 0:Wd], in_=e32[ei][:, 0:Wd], func=AF.Ln, bias=1.0),
                     [Re32[ei]], [Rlp[li]])

            def stageB(ti, t):
                hp, hh, kb, Q = t["hp"], t["hh"], t["kb"], t["Q"]
                i = hp % 2
                off, Wd, t0, diag = geom(t)
                r0 = hh * 64
                zb, li, ei = t["zb"], t["lp"], t["e"]
                si = t["row"] % 2
                ai = ti % 3
                xi = ti % 2
                ob = 4 + (t["Q"] % 2)
                if t["first"]:
                    k.op("dve", lambda e: e.memset(Sb[si], 0.0), [], [RSb[si]])
                MM(ps[zb][:, 0:Wd], NLm, lpb[li][:, 0:Wd], True, t["first"], [Rlp[li], Rc], [Rps[zb]])
                if not t["first"]:
                    MM(ps[zb][:, 0:Wd], NEG1, Sb[si][:, off:512], False, True, [RSb[si], Rc], [Rps[zb]])
                k.op("act", lambda e: e.activation(out=ex32[xi][:, 0:Wd], in_=ps[zb][:, 0:Wd], func=AF.Exp),
                     [Rps[zb]], [Rex[xi]])
                k.op("dve", lambda e: e.tensor_tensor(out=ATb[ai][:, 0:Wd], in0=ex32[xi][:, 0:Wd], in1=e32[ei][:, 0:Wd],
                                                      op=ALU.mult), [Rex[xi], Re32[ei]], [RAT[ai]])
                MM(ps[ob][r0:r0 + 64, off:512], Vp[i][:, kb, r0:r0 + 64], ATb[ai][:, 0:Wd], t["first"], t["last"],
                   [RVp[i], RAT[ai]], [Rps[ob]], inc=True, skip_group_check=True)
                if not t["last"]:
                    k.op("dve", lambda e: e.tensor_tensor(out=Sb[si][:, off:512], in0=Sb[si][:, off:512],
                                                          in1=lpb[li][:, 0:Wd], op=ALU.add),
                         [RSb[si], Rlp[li]], [RSb[si]])
                elif hh == 1:
                    k.op("dve", lambda e: e.tensor_copy(out=osT[:, hp, Q * 512:(Q + 1) * 512], in_=ps[ob][:, :]),
                         [Rps[ob]], [Ros[hp][Q]])

            nt = len(tiles)
            for ti in range(nt + 1):
                if ti < nt:
                    stageA(ti, tiles[ti])
                if ti >= 1:
                    stageB(ti - 1, tiles[ti - 1])
                if ti == 8:
                    for q in range(2):
                        k.dma("pool", wps[:, q * 4:(q + 1) * 4, :],
                              w_psb_d[q * 512:(q + 1) * 512, :].rearrange("(fc p) n -> p fc n", p=128),
                              writes=[Rwps], tgt=Rwps)
                        k.dma("pool", wbs[:, :, q * 512:(q + 1) * 512],
                              wview(w_in_d[:, 9216 + q * 512:9216 + (q + 1) * 512]), writes=[Rwbs], tgt=Rwbs)
            dump('osb', osT, b, 'D')
            gtmp = [view(184 * KB + i * 2 * KB, [128, 512], F32) for i in range(2)]
            ptmp = [view(188 * KB + i * 2 * KB, [128, 512], F32) for i in range(2)]
            Rgt = [Reg(f"gtE{b}_{i}") for i in range(2)]
            Rpt = [Reg(f"ptE{b}_{i}") for i in range(2)]
            it = 0
            for ec in range(8):
                for tg in range(4):
                    i = it % 2
                    it += 1
                    P_, G_ = 2 * i, 2 * i + 1
                    tsl = slice(tg * 512, (tg + 1) * 512)
                    for fc in range(8):
                        MM(ps[P_][:, :], wps[:, fc, ec * 128:(ec + 1) * 128], osT[:, fc, tsl], fc == 0, fc == 7,
                           [Rwps, Ros[fc][tg]], [Rps[P_]])
                    for kc in range(8):
                        MM(ps[G_][:, :], wbs[:, kc, ec * 128:(ec + 1) * 128], hT[:, kc, tsl], kc == 0, kc == 7,
                           [Rwbs] + hregs(kc, tg * 4, tg * 4 + 4), [Rps[G_]])
                    k.op("act", lambda e, i=i, G_=G_, ec=ec: e.activation(
                        out=gtmp[i], in_=ps[G_][:, :], func=AF.Sigmoid, bias=bbrT[:, ec:ec + 1]),
                        [Rps[G_], Rc], [Rgt[i]])
                    k.op("dve", lambda e, i=i, P_=P_: e.tensor_tensor(
                        out=ptmp[i], in0=ps[P_][:, :], in1=gtmp[i], op=ALU.mult), [Rps[P_], Rgt[i]], [Rpt[i]])
                    k.op("pool", lambda e, i=i, ec=ec, tsl=tsl: e.tensor_tensor(
                        out=mT[:, ec, tsl], in0=mT[:, ec, tsl], in1=ptmp[i], op=ALU.add),
                        [Rpt[i], RmT[ec][tg]], [RmT[ec][tg]])
            k.barrier()

            dump('merged', mT, b, 'E')
            x1 = view(64 * KB, [128, 16, D], F32)
            Rx1 = [Reg(f"x1{b}_{n}") for n in range(16)]
            wout = view(128 * KB, [128, 8, D], BF16)
            Rwout = Reg(f"wout{b}")
            xbuf = [view(144 * KB + i * 4 * KB, [128, D], F32) for i in range(2)]
            Rxb = [Reg(f"xbF{b}_{i}") for i in range(2)]
            ytmp = [view(152 * KB + i * 2 * KB, [128, 512], F32) for i in range(2)]
            Ryt = [Reg(f"yt{b}_{i}") for i in range(2)]
            xnb = [view(156 * KB + i * 4 * KB, [128, D], F32) for i in range(2)]
            Rxn = [Reg(f"xnF{b}_{i}") for i in range(2)]
            junk = view(164 * KB, [128, D], BF16)
            Rjunk = Reg(f"junkF{b}")
            gbc = view(166 * KB, [128, 2, D], F32)
            Rgbc = Reg(f"gbc{b}")
            for q in range(2):
                k.dma("pool", wout[:, q * 4:(q + 1) * 4, :],
                      w_out_d[q * 512:(q + 1) * 512, :].rearrange("(fc p) n -> p fc n", p=128),
                      writes=[Rwout], tgt=Rwout)
            Rdg = [Reg(f"dg{b}_{i}") for i in range(2)]
            cnt = 0
            for gi in range(2):
                for hf in range(2):
                    pb_ = 4 + (gi * 2 + hf) % 2
                    for c4 in range(4):
                        chunk = 16 + 24 * gi + hf * 4 + c4
                        di = cnt % 2
                        cnt += 1
                        k.op("dve", lambda e, di=di, chunk=chunk: e.tensor_scalar(
                            out=dgt[:, di, :], in0=identF, scalar1=modT3[:, chunk, b:b + 1], scalar2=None,
                            op0=ALU.mult), [Rc, Rmod], [Rdg[di]])
                        MM(ps[pb_][:, c4 * 128:(c4 + 1) * 128], onesF, dgt[:, di, :], True, True,
                           [Rc, Rdg[di]], [Rps[pb_]])
                    k.op("act", lambda e, gi=gi, hf=hf, pb_=pb_: e.copy(
                        out=gbc[:, gi, hf * 512:(hf + 1) * 512], in_=ps[pb_][:, :]), [Rps[pb_]], [Rgbc])
            for n in range(16):
                i = n % 2
                tsl = slice(n * 128, (n + 1) * 128)
                k.dma("sp", xbuf[i], x_d[b, tsl, :], writes=[Rxb[i]], tgt=Rxb[i])
                for eh in range(2):
                    pb_ = 4 + eh
                    for fc in range(8):
                        MM(ps[pb_][:, :], mT[:, fc, tsl], wout[:, fc, eh * 512:(eh + 1) * 512], fc == 0, fc == 7,
                           [Rwout, RmT[fc][n // 4]], [Rps[pb_]])
                    k.op("dve", lambda e, eh=eh, pb_=pb_: e.tensor_tensor(
                        out=ytmp[eh], in0=ps[pb_][:, :], in1=gbc[:, 0, eh * 512:(eh + 1) * 512], op=ALU.mult),
                        [Rps[pb_], Rgbc], [Ryt[eh]])
                    k.op("pool", lambda e, eh=eh, n=n, i=i: e.tensor_tensor(
                        out=x1[:, n, eh * 512:(eh + 1) * 512], in0=ytmp[eh], in1=xbuf[i][:, eh * 512:(eh + 1) * 512],
                        op=ALU.add), [Ryt[eh], Rxb[i]], [Rx1[n]])
                norm_to_hT(x1[:, n, :], Rx1[n], n, b, 1, xnb[i], Rxn[i], junk, Rjunk, (0 + 2 * i, 1 + 2 * i))
            k.barrier()

            dump('hffn', hT, b, 'F')
            dump('x1', x1, b, 'F2')
            groups = [(0, 4), (4, 4), (8, 4), (12, 4), (16, 4), (20, 2)]
            wsets = []
            for i, base in enumerate((32 * KB, 128 * KB)):
                wsets.append(dict(a=view(base, [128, 8, 512], BF16), u=view(base + 8 * KB, [128, 8, 512], BF16),
                                  w2=view(base + 16 * KB, [128, 4, D], BF16),
                                  Ra=Reg(f"w1a{b}_{i}"), Ru=Reg(f"w1u{b}_{i}"), R2=Reg(f"w2{b}_{i}")))
            hTg = [view(152 * KB + i * 4 * KB, [128, 4, 512], BF16) for i in range(2)]
            RhTg = [Reg(f"hTg{b}_{i}") for i in range(2)]
            sab = [view(160 * KB + i * KB, [128, 512], BF16) for i in range(2)]
            Rsa = [Reg(f"sa{b}_{i}") for i in range(2)]
            it = 0
            ob_i = 0
            for gidx, (f0, G) in enumerate(groups):
                ws = wsets[gidx % 2]
                k.dma("pool", ws["a"][:, :, 0:G * 128], wview(w_f1_d[:, f0 * 128:(f0 + G) * 128]),
                      writes=[ws["Ra"]], tgt=ws["Ra"])
                k.dma("pool", ws["u"][:, :, 0:G * 128], wview(w_f1_d[:, DFF + f0 * 128:DFF + (f0 + G) * 128]),
                      writes=[ws["Ru"]], tgt=ws["Ru"])
                k.dma("pool", ws["w2"][:, 0:G, :],
                      w_f2_d[f0 * 128:(f0 + G) * 128, :].rearrange("(g p) n -> p g n", p=128),
                      writes=[ws["R2"]], tgt=ws["R2"])
                for g in range(G):
                    k.op("pool", lambda e, ws=ws, g=g: e.tensor_tensor(
                        out=ws["w2"][:, g, :], in0=ws["w2"][:, g, :], in1=gbc[:, 1, :], op=ALU.mult),
                        [ws["R2"], Rgbc], [ws["R2"]])
                for tg in range(4):
                    hi = it % 2
                    it += 1
                    tsl = slice(tg * 512, (tg + 1) * 512)
                    for g in range(G):
                        si = g % 2
                        pa, pu = 0 + si, 2 + si
                        for kc in range(8):
                            MM(ps[pa][:, :], ws["a"][:, kc, g * 128:(g + 1) * 128], hT[:, kc, tsl], kc == 0, kc == 7,
                               [ws["Ra"]] + hregs(kc, tg * 4, tg * 4 + 4), [Rps[pa]])
                        for kc in range(8):
                            MM(ps[pu][:, :], ws["u"][:, kc, g * 128:(g + 1) * 128], hT[:, kc, tsl], kc == 0, kc == 7,
                               [ws["Ru"]] + hregs(kc, tg * 4, tg * 4 + 4), [Rps[pu]])
                        k.op("act", lambda e, si=si, pa=pa: e.activation(out=sab[si], in_=ps[pa][:, :], func=AF.Silu),
                             [Rps[pa]], [Rsa[si]])
                        k.op("dve", lambda e, si=si, pu=pu, hi=hi, g=g: e.tensor_tensor(
                            out=hTg[hi][:, g, :], in0=ps[pu][:, :], in1=sab[si], op=ALU.mult),
                            [Rps[pu], Rsa[si]], [RhTg[hi]])
                    for tb in range(4):
                        n = tg * 4 + tb
                        for eh in range(2):
                            pb_ = 4 + ob_i % 4
                            ob_i += 1
                            for g in range(G):
                                MM(ps[pb_][:, :], hTg[hi][:, g, tb * 128:(tb + 1) * 128],
                                   ws["w2"][:, g, eh * 512:(eh + 1) * 512], g == 0, g == G - 1,
                                   [RhTg[hi], ws["R2"]], [Rps[pb_]])
                            k.op("dve", lambda e, n=n, eh=eh, pb_=pb_: e.tensor_tensor(
                                out=x1[:, n, eh * 512:(eh + 1) * 512], in0=ps[pb_][:, :],
                                in1=x1[:, n, eh * 512:(eh + 1) * 512], op=ALU.add), [Rps[pb_], Rx1[n]], [Rx1[n]])
            k.barrier()

            dump('x1', x1, b, 'G') if False else None
            if STOP == 'G':
                raise _Stop()
            gfin = view(32 * KB, [128, D], F32)
            Rgf = Reg(f"gfin{b}")
            junk = view(36 * KB, [128, D], BF16)
            Rjunk = Reg(f"junkH{b}")
            k.dma("sp", gfin, gfin_d, writes=[Rgf], tgt=Rgf)
            for n in range(16):
                rstd, Rr = rms_rstd(x1[:, n, :], Rx1[n], junk, Rjunk, D)
                k.op("dve", lambda e, n=n, rstd=rstd: e.scalar_tensor_tensor(
                    out=x1[:, n, :], in0=x1[:, n, :], scalar=rstd, in1=gfin, op0=ALU.mult, op1=ALU.mult),
                    [Rx1[n], Rr, Rgf], [Rx1[n]])
                k.dma("sp", out_d[b, n * 128:(n + 1) * 128, :], x1[:, n, :], reads=[Rx1[n]], tgt=Rout)
            k.barrier()
        try:
            main_loop()
        except _Stop:
            pass
    return nc


_CACHE = {}


def _consts():
    bf = ml_dtypes.bfloat16
    idx = np.arange(128)
    identF = np.eye(128, dtype=np.float32)
    cb16 = np.zeros((128, 4, 128), dtype=np.float32)
    cb16[:, 0, :] = identF
    cb16[:, 1, :] = -(idx[:, None] >= idx[None, :]).astype(np.float32)
    cb16[:, 2, :] = -1.0
    cb16[:, 3, :] = (idx[:, None] < idx[None, :]).astype(np.float32)
    cb16 = cb16.astype(bf)
    inv_freq = (np.float32(10000.0) ** (-np.arange(0, 256, 2, dtype=np.float32) / np.float32(256))).astype(np.float32)
    ang = (inv_freq[:, None] * np.arange(S, dtype=np.float32)[None, :]).astype(np.float32)
    cossin = np.stack([np.cos(ang.astype(np.float64)), np.sin(ang.astype(np.float64))], axis=1).astype(np.float32)
    lg = np.log(1.0 - 2.0 ** (-5.0 - np.arange(4, dtype=np.float64)))
    dmaskT = np.zeros((128, 4, 128), dtype=np.float64)
    rdec = np.zeros((128, 8), dtype=np.float64)
    for h in range(4):
        diff = idx[None, :] - idx[:, None]
        dmaskT[:, h, :] = np.where(diff >= 0, np.exp(np.maximum(diff, 0) * lg[h]), 0.0) / 16.0
        rdec[:, h] = np.exp((idx + 1.0) * lg[h])
        rdec[:, 4 + h] = np.exp((127.0 - idx) * lg[h]) / 16.0
    identF = np.ascontiguousarray(np.stack([identF, np.ones((128, 128), np.float32)], axis=1))
    return dict(identF=identF, cb16=cb16, cossin=np.ascontiguousarray(cossin),
                dmaskT=dmaskT.astype(np.float32), rdec=rdec.astype(np.float32))


def kernel(x, c, w_ada, b_ada, g_norm1, w_in, b_branch, w_proj_sb, w_proj_ret, w_out, g_norm2,
           w_ffn_in, w_ffn_out, g_final):
    f = lambda a: np.ascontiguousarray(np.asarray(a, dtype=np.float32))
    x, c = f(x), f(c)
    if "nc" not in _CACHE:
        _CACHE["nc"] = build_program()
        _CACHE["consts"] = _consts()
    nc = _CACHE["nc"]
    cst = _CACHE["consts"]
    b_ada0 = f(b_ada)[0]
    shared = dict(
        w_ada=f(w_ada)[0], w_in=f(w_in)[0], w_proj_sb=f(w_proj_sb)[0], w_proj_ret=f(w_proj_ret)[0],
        w_out=f(w_out)[0], w_ffn_in=f(w_ffn_in)[0], w_ffn_out=f(w_ffn_out)[0],
        badaT2=np.ascontiguousarray(np.repeat(b_ada0.reshape(48, 128).T[:, :, None], NB, axis=2).reshape(128, 96)),
        gT=np.ascontiguousarray(np.concatenate([f(g_norm1)[0].reshape(8, 128).T, f(g_norm2)[0].reshape(8, 128).T], axis=1)),
        bbrT=np.ascontiguousarray(f(b_branch)[0].reshape(16, 128).T),
        gfin=np.ascontiguousarray(np.broadcast_to(f(g_final)[None, :], (128, D))),
        **cst)
    in_maps = []
    for i in range(NCORES):
        m = dict(shared)
        m["x"] = np.ascontiguousarray(x[i * NB:(i + 1) * NB])
        cc = c[i * NB:(i + 1) * NB]
        m["cT"] = np.ascontiguousarray(cc.T.reshape(8, 128, NB).transpose(1, 0, 2))
        in_maps.append(m)
    res = run_bass_kernel_spmd(nc, in_maps, core_ids=list(range(NCORES)))
    return np.concatenate([np.asarray(r["out"]) for r in res.results], axis=0).astype(np.float32)
```
